# Optimizing a Trainium2 kernel written in Bass

```python
import math
import numpy as np
import jax
import jax.numpy as jnp
from jax import lax

D_MODEL = 2048
BATCH = 4
SEQ = 4096
DEPTH = 2

N_MIXERS = 2
N_A = (DEPTH + 1) // 2
N_B = DEPTH // 2
BLK = 128
EPS = 1e-6

MLA_HEADS = 16
Q_LORA = 512
KV_LORA = 512
NOPE_DIM = 128
ROPE_DIM = 64
V_DIM = 128
QK_DIM = NOPE_DIM + ROPE_DIM
ROPE_THETA = 10000.0

DIL_PAIRS = ((128, 1), (512, 4), (2048, 16))
DIL_GROUPS = len(DIL_PAIRS)
DIL_HEADS = 8
DIL_HEAD_DIM = 128
ALIBI_TOTAL_HEADS = DIL_GROUPS * DIL_HEADS

D_FF = 5632

kernel_name = "hybrid_mla_dilated_macaron"


def rmsnorm(t, g):
    tf = t.astype(jnp.float32)
    y = tf * lax.rsqrt(jnp.mean(tf * tf, axis=-1, keepdims=True) + EPS)
    return (y * g.astype(jnp.float32)).astype(t.dtype)


def swiglu(xn, w_in, w_out):
    gate, up = jnp.split(xn @ w_in, 2, axis=-1)
    return (jax.nn.silu(gate) * up) @ w_out


def rope_tables(S):
    inv = 1.0 / (ROPE_THETA ** (jnp.arange(0, ROPE_DIM, 2, dtype=jnp.float32) / ROPE_DIM))
    ang = jnp.arange(S, dtype=jnp.float32)[:, None] * inv[None, :]
    return jnp.cos(ang), jnp.sin(ang)


def apply_rope(t, cos, sin):
    t1, t2 = jnp.split(t, 2, axis=-1)
    c = cos[None, :, None, :].astype(t.dtype)
    s = sin[None, :, None, :].astype(t.dtype)
    return jnp.concatenate([t1 * c - t2 * s, t1 * s + t2 * c], axis=-1)


def mla_mixer(xn, w_down, g_cq, g_ckv, w_uq, w_ukv, g_qn, g_kn, w_o):
    B, S, _ = xn.shape
    lat = xn @ w_down
    c_q = rmsnorm(lat[..., :Q_LORA], g_cq)
    c_kv = rmsnorm(lat[..., Q_LORA:Q_LORA + KV_LORA], g_ckv)
    k_pe = lat[..., Q_LORA + KV_LORA:]
    q = (c_q @ w_uq).reshape(B, S, MLA_HEADS, QK_DIM)
    kv = (c_kv @ w_ukv).reshape(B, S, MLA_HEADS, NOPE_DIM + V_DIM)
    k_nope, v = kv[..., :NOPE_DIM], kv[..., NOPE_DIM:]
    k = jnp.concatenate(
        [k_nope, jnp.broadcast_to(k_pe[:, :, None, :], (B, S, MLA_HEADS, ROPE_DIM))], axis=-1)
    q = rmsnorm(q, g_qn)
    k = rmsnorm(k, g_kn)
    cos, sin = rope_tables(S)
    q = jnp.concatenate([q[..., :NOPE_DIM], apply_rope(q[..., NOPE_DIM:], cos, sin)], axis=-1)
    k = jnp.concatenate([k[..., :NOPE_DIM], apply_rope(k[..., NOPE_DIM:], cos, sin)], axis=-1)

    scale = 1.0 / math.sqrt(QK_DIM)
    nb = S // BLK
    qb = q.reshape(B, nb, BLK, MLA_HEADS, QK_DIM).transpose(1, 0, 2, 3, 4)
    starts = jnp.arange(nb, dtype=jnp.int32) * BLK
    kf = k.astype(jnp.float32)
    vf = v.astype(jnp.float32)
    kpos = jnp.arange(S, dtype=jnp.int32)

    def block(args):
        qblk, s0 = args
        s = jnp.einsum('bqhe,bkhe->bhqk', qblk.astype(jnp.float32), kf) * scale
        qpos = s0 + jnp.arange(BLK, dtype=jnp.int32)
        s = jnp.where((kpos[None, :] <= qpos[:, None])[None, None], s, -jnp.inf)
        p = jax.nn.softmax(s, axis=-1)
        return jnp.einsum('bhqk,bkhe->bqhe', p, vf)

    o = lax.map(block, (qb, starts))
    o = o.transpose(1, 0, 2, 3, 4).reshape(B, S, MLA_HEADS * V_DIM).astype(xn.dtype)
    return o @ w_o


def dilated_group_attn(q, k, v, window, dilation, slopes):
    B, S, H, dh = q.shape
    win_sub = window // dilation
    L = S // dilation
    Lp = -(-L // BLK) * BLK
    nb = Lp // BLK

    def to_sub(t):
        t = t.astype(jnp.float32).reshape(B, L, dilation, H, dh).transpose(0, 2, 1, 3, 4)
        return jnp.pad(t, ((0, 0), (0, 0), (0, Lp - L), (0, 0), (0, 0)))

    qb = to_sub(q).reshape(B, dilation, nb, BLK, H, dh)
    kb = to_sub(k).reshape(B, dilation, nb, BLK, H, dh)
    vb = to_sub(v).reshape(B, dilation, nb, BLK, H, dh)
    pad_prev = ((0, 0), (0, 0), (1, 0), (0, 0), (0, 0), (0, 0))
    kcat = jnp.concatenate([jnp.pad(kb, pad_prev)[:, :, :-1], kb], axis=3)
    vcat = jnp.concatenate([jnp.pad(vb, pad_prev)[:, :, :-1], vb], axis=3)

    iq = jnp.arange(BLK, dtype=jnp.int32)
    ik = jnp.arange(2 * BLK, dtype=jnp.int32)
    nidx = jnp.arange(nb, dtype=jnp.int32)
    dist = iq[:, None] + BLK - ik[None, :]
    key_ok = (nidx[:, None] * BLK - BLK + ik[None, :]) >= 0
    mask = ((dist >= 0) & (dist <= win_sub))[None] & key_ok[:, None, :]
    bias = -slopes[:, None, None] * (dilation * dist).astype(jnp.float32)[None]

    scale = 1.0 / math.sqrt(dh)
    s = jnp.einsum('bdnqhe,bdnkhe->bdnhqk', qb, kcat) * scale + bias[None, None, None]
    s = jnp.where(mask[None, None, :, None], s, -jnp.inf)
    lse = jax.nn.logsumexp(s, axis=-1)
    p = jnp.exp(s - lse[..., None])
    o = jnp.einsum('bdnhqk,bdnkhe->bdnqhe', p, vcat)

    o = o.reshape(B, dilation, Lp, H, dh)[:, :, :L].transpose(0, 2, 1, 3, 4).reshape(B, S, H, dh)
    lse = lse.transpose(0, 1, 2, 4, 3).reshape(B, dilation, Lp, H)[:, :, :L]
    lse = lse.transpose(0, 2, 1, 3).reshape(B, S, H)
    return o, lse


def alibi_slopes():
    k = np.arange(1, ALIBI_TOTAL_HEADS + 1, dtype=np.float32)
    return jnp.asarray(2.0 ** (-8.0 * k / ALIBI_TOTAL_HEADS), dtype=jnp.float32)


def dilated_mixer(xn, w_qkv, g_qn, g_kn, w_o):
    B, S, _ = xn.shape
    qkv = (xn @ w_qkv).reshape(B, S, 3, DIL_GROUPS, DIL_HEADS, DIL_HEAD_DIM)
    q = rmsnorm(qkv[:, :, 0], g_qn)
    k = rmsnorm(qkv[:, :, 1], g_kn)
    v = qkv[:, :, 2]
    slopes = alibi_slopes().reshape(DIL_GROUPS, DIL_HEADS)
    outs, lses = [], []
    for g, (window, dilation) in enumerate(DIL_PAIRS):
        o_g, lse_g = dilated_group_attn(q[:, :, g], k[:, :, g], v[:, :, g], window, dilation, slopes[g])
        outs.append(o_g)
        lses.append(lse_g)
    o = jnp.stack(outs, axis=2)
    w = jax.nn.softmax(jnp.stack(lses, axis=2), axis=2)
    o = jnp.sum(o * w[..., None], axis=2).reshape(B, S, DIL_HEADS * DIL_HEAD_DIM).astype(xn.dtype)
    return o @ w_o


def setup_inputs(seed: int = 0) -> dict:
    key = jax.random.key(seed)
    ks = iter(jax.random.split(key, 40))

    def w(shape, fan_in):
        return jax.random.normal(next(ks), shape, jnp.float32) * (fan_in ** -0.5)

    def gain(shape):
        return 1.0 + 0.02 * jax.random.normal(next(ks), shape, jnp.float32)

    D = D_MODEL
    return {
        "x": jax.random.normal(next(ks), (BATCH, SEQ, D), jnp.float32),
        "ffn1_norm": gain((DEPTH, D)),
        "ffn1_w_in": w((DEPTH, D, 2 * D_FF), D),
        "ffn1_w_out": w((DEPTH, D_FF, D), D_FF),
        "mix_norm": gain((DEPTH, D)),
        "ffn2_norm": gain((DEPTH, D)),
        "ffn2_w_in": w((DEPTH, D, 2 * D_FF), D),
        "ffn2_w_out": w((DEPTH, D_FF, D), D_FF),
        "mla_w_down": w((N_A, D, Q_LORA + KV_LORA + ROPE_DIM), D),
        "mla_g_cq": gain((N_A, Q_LORA)),
        "mla_g_ckv": gain((N_A, KV_LORA)),
        "mla_w_uq": w((N_A, Q_LORA, MLA_HEADS * QK_DIM), Q_LORA),
        "mla_w_ukv": w((N_A, KV_LORA, MLA_HEADS * (NOPE_DIM + V_DIM)), KV_LORA),
        "mla_g_qn": gain((N_A, QK_DIM)),
        "mla_g_kn": gain((N_A, QK_DIM)),
        "mla_w_o": w((N_A, MLA_HEADS * V_DIM, D), MLA_HEADS * V_DIM),
        "dil_w_qkv": w((N_B, D, 3 * DIL_GROUPS * DIL_HEADS * DIL_HEAD_DIM), D),
        "dil_g_qn": gain((N_B, DIL_HEAD_DIM)),
        "dil_g_kn": gain((N_B, DIL_HEAD_DIM)),
        "dil_w_o": w((N_B, DIL_HEADS * DIL_HEAD_DIM, D), DIL_HEADS * DIL_HEAD_DIM),
    }


def reference(x, ffn1_norm, ffn1_w_in, ffn1_w_out, mix_norm, ffn2_norm, ffn2_w_in, ffn2_w_out,
              mla_w_down, mla_g_cq, mla_g_ckv, mla_w_uq, mla_w_ukv, mla_g_qn, mla_g_kn, mla_w_o,
              dil_w_qkv, dil_g_qn, dil_g_kn, dil_w_o):
    for i in range(DEPTH):
        j = i // N_MIXERS
        x = x + 0.5 * swiglu(rmsnorm(x, ffn1_norm[i]), ffn1_w_in[i], ffn1_w_out[i])
        xn = rmsnorm(x, mix_norm[i])
        if i % N_MIXERS == 0:
            y = mla_mixer(xn, mla_w_down[j], mla_g_cq[j], mla_g_ckv[j], mla_w_uq[j], mla_w_ukv[j],
                          mla_g_qn[j], mla_g_kn[j], mla_w_o[j])
        else:
            y = dilated_mixer(xn, dil_w_qkv[j], dil_g_qn[j], dil_g_kn[j], dil_w_o[j])
        x = x + y
        x = x + 0.5 * swiglu(rmsnorm(x, ffn2_norm[i]), ffn2_w_in[i], ffn2_w_out[i])
    return x
```

```python
import numpy as np
import concourse.bass as bass
import concourse.mybir as mybir
from concourse.bass_utils import run_bass_kernel_spmd

F32 = mybir.dt.float32
BF16 = mybir.dt.bfloat16
AF = mybir.ActivationFunctionType
ALU = mybir.AluOpType

SAME_ENGINE_SYNC = False


class T:
    __slots__ = ("name", "w", "r", "sem", "semcnt")

    def __init__(self, name=""):
        self.name = name
        self.w = None
        self.r = []
        self.sem = None
        self.semcnt = 0


class Op:
    __slots__ = ("q", "fn", "deps", "idx", "needs_inc", "semval", "dma", "semtile", "ninc", "sem", "incamt")


class Prog:
    QUEUES = ("pe", "act", "dve", "pool", "sp")

    def __init__(self, nc):
        self.nc = nc
        self.ops = []
        self.nsem = 0
        self.engsem = {q: nc.alloc_semaphore("s_" + q) for q in self.QUEUES}
        self.engcnt = {q: 0 for q in self.QUEUES}
        self.last_compute = {q: None for q in self.QUEUES}
        self.live_dma_tiles = []
        self.free_sems = []

    def _record(self, q, fn, reads, writes, dma, semtile, n=1, extra_deps=(), incamt=16):
        op = Op()
        op.q = q
        op.fn = fn
        op.idx = len(self.ops)
        op.dma = dma
        op.semtile = semtile
        op.needs_inc = dma
        op.semval = None
        op.ninc = n
        op.incamt = incamt
        deps = set(extra_deps)
        for t in reads:
            if t.w is not None:
                deps.add(t.w)
        for t in writes:
            if t.w is not None:
                deps.add(t.w)
            deps.update(t.r)
        for t in reads:
            t.r.append(op.idx)
        for t in writes:
            t.w = op.idx
            t.r = []
        pruned = []
        for d in deps:
            dop = self.ops[d]
            if (not dop.dma) and (not dma) and dop.q == q and fn is not None:
                if q == "pe" or not SAME_ENGINE_SYNC:
                    continue
            if not dop.dma and not dop.needs_inc:
                dop.needs_inc = True
            pruned.append(d)
        op.deps = pruned
        if dma:
            t = semtile
            if t.sem is None:
                if self.free_sems:
                    t.sem, t.semcnt = self.free_sems.pop()
                else:
                    t.sem = self.nc.alloc_semaphore("d_%d" % self.nsem)
                    t.semcnt = 0
                    self.nsem += 1
                self.live_dma_tiles.append(t)
            t.semcnt += incamt * n
            op.semval = t.semcnt
            op.sem = t.sem
        else:
            op.sem = self.engsem[q]
            if fn is not None:
                self.last_compute[q] = op.idx
        self.ops.append(op)
        return op

    def op(self, q, fn, reads=(), writes=()):
        return self._record(q, fn, reads, writes, False, None)

    def dma(self, q, fn, reads=(), writes=(), semtile=None, n=1):
        assert semtile is not None
        return self._record(q, fn, reads, writes, True, semtile, n=n)

    def cc(self, fn, reads=(), writes=(), semtile=None):
        return self._record("pool", fn, reads, writes, True, semtile, n=1, incamt=1)

    def fence(self):
        deps = [i for i in self.last_compute.values() if i is not None]
        last_dma = {}
        for op in self.ops:
            if op.dma:
                last_dma[op.sem.num] = op.idx
        deps += list(last_dma.values())
        for q in self.QUEUES:
            self._record(q, None, (), (), False, None, extra_deps=deps)
        for t in self.live_dma_tiles:
            self.free_sems.append((t.sem, t.semcnt))
            t.sem = None
        self.live_dma_tiles = []

    def emit(self):
        nc = self.nc
        ops = self.ops
        cnt = {q: 0 for q in self.QUEUES}
        for op in ops:
            if (not op.dma) and op.needs_inc and op.fn is not None:
                cnt[op.q] += 1
                op.semval = cnt[op.q]
        per_q = {q: [] for q in self.QUEUES}
        for op in ops:
            per_q[op.q].append(op)
        final = {}
        for op in ops:
            if op.dma:
                final[op.sem.num] = (op.sem, op.semval)
        engsem = self.engsem

        def run_queue(q, eng):
            seen = {}
            for op in per_q[q]:
                waits = {}
                for d in op.deps:
                    dop = ops[d]
                    if dop.semval is None:
                        continue
                    sem = dop.sem
                    key = sem.num
                    v = dop.semval
                    if seen.get(key, 0) >= v:
                        continue
                    if key not in waits or waits[key][1] < v:
                        waits[key] = (sem, v)
                for key, (sem, v) in waits.items():
                    eng.wait_ge(sem, v)
                    seen[key] = v
                if op.fn is None:
                    continue
                if op.dma:
                    instrs = op.fn(eng)
                    assert len(instrs) == op.ninc, (len(instrs), op.ninc)
                    for ins in instrs:
                        ins.then_inc(op.sem, op.incamt)
                else:
                    ins = op.fn(eng)
                    if op.needs_inc:
                        ins.then_inc(engsem[q], 1)
            if q == "sp":
                for key, (sem, v) in final.items():
                    if seen.get(key, 0) < v:
                        eng.wait_ge(sem, v)
                for qq in self.QUEUES:
                    if cnt[qq] > 0:
                        eng.wait_ge(engsem[qq], cnt[qq])

        with nc.Block() as block:
            @block.tensor
            def _(e):
                run_queue("pe", e)

            @block.scalar
            def _(e):
                run_queue("act", e)

            @block.vector
            def _(e):
                run_queue("dve", e)

            @block.gpsimd
            def _(e):
                run_queue("pool", e)

            @block.sync
            def _(e):
                run_queue("sp", e)


class Rot:
    def __init__(self, items):
        self.items = items
        self.i = 0

    def next(self):
        it = self.items[self.i % len(self.items)]
        self.i += 1
        return it


class PsRot(Rot):
    def next(self):
        it = Rot.next(self)
        t = it[1]
        assert t.w is None or len(t.r) > 0, "PSUM bank re-handed before being read"
        return it


class Arena:
    def __init__(self, nc, base=16512, limit=229344):
        self.nc = nc
        self.off = base
        self.base = base
        self.limit = limit
        self.n = 0
        self.peak = 0

    def alloc(self, shape, dtype, name=None):
        esz = 2 if dtype == BF16 else 4
        per_part = esz
        for s in shape[1:]:
            per_part *= s
        per_part = (per_part + 63) // 64 * 64
        self.n += 1
        nm = "%s_%d" % (name or "sb", self.n)
        h = self.nc.alloc_sbuf_tensor_at(nm, list(shape), dtype, offset=self.off)
        self.off += per_part
        self.peak = max(self.peak, self.off)
        assert self.off <= self.limit, ("SBUF arena overflow", self.off, self.limit)
        return h

    def mark(self):
        return self.off

    def reset(self, m):
        self.off = m


class Ctx:
    def __init__(self, nc):
        self.nc = nc
        self.P = Prog(nc)
        self.A = Arena(nc)
        self.banks = [(nc.alloc_psum_tensor("ps%d" % k, [128, 512], F32), T("ps%d" % k))
                      for k in range(8)]
        self.ps = PsRot(self.banks)
        A = self.A
        self.ones_bf = A.alloc([128, 128], BF16, "ones")
        self.ones_T = T("ones")
        ones = self.ones_bf
        self.P.op("pool", lambda e: e.memset(ones[:, :], 1.0), writes=[self.ones_T])
        self.eps_sb = A.alloc([128, 1], F32, "eps")
        self.eps_T = T("eps")
        eps_sb = self.eps_sb
        self.P.op("pool", lambda e: e.memset(eps_sb[:, :], 1e-6), writes=[self.eps_T])


class NormScratch:
    def __init__(self, cx, DC):
        A = cx.A
        self.xs = A.alloc([128, DC, 512], F32, "xs"); self.xs_T = T("xs")
        self.sq = Rot([(A.alloc([128, 512], BF16, "sq"), T("sq")) for _ in range(2)])
        self.rt = A.alloc([128, 512], F32, "rt"); self.rt_T = T("rt")
        self.rstd = A.alloc([128, 512], F32, "rstd"); self.rstd_T = T("rstd")


def rstd_from_ss(cx, st, ssb, ssb_T, n, rows=128):
    P = cx.P
    rt, rstd = st.rt, st.rstd
    eps = cx.eps_sb
    P.op("act", lambda e: e.activation(out=rt[0:rows, :], in_=ssb[0:rows, :], func=AF.Sqrt,
                                       bias=eps[0:rows, 0:1], scale=1.0 / n),
         reads=[ssb_T, cx.eps_T], writes=[st.rt_T])
    P.op("dve", lambda e: e.reciprocal(out=rstd[0:rows, :], in_=rt[0:rows, :]),
         reads=[st.rt_T], writes=[st.rstd_T])


def norm_load(cx, st, x_in, t0):
    xs, xs_T = st.xs, st.xs_T
    cx.P.dma("sp", lambda e: [e.dma_start(out=xs[:, :, :], in_=x_in[:, :, t0:t0 + 512])],
             writes=[xs_T], semtile=xs_T)


def norm_subtile(cx, st, x_in, t0, g_sb, g_T, DC, xn, xn_T, col0, load=True):
    P = cx.P
    xs, xs_T = st.xs, st.xs_T
    ones = cx.ones_bf
    D = DC * 128
    if load:
        norm_load(cx, st, x_in, t0)
    ssb, ssb_T = cx.ps.next()
    for c in range(DC):
        sqb, sqb_T = st.sq.next()
        P.op("act", lambda e, c=c, sqb=sqb: e.activation(out=sqb[:, :], in_=xs[:, c, :], func=AF.Square),
             reads=[xs_T], writes=[sqb_T])
        P.op("pe", lambda e, c=c, sqb=sqb: e.matmul(ssb[:, :], lhsT=ones[:, :], rhs=sqb[:, :],
                                                    start=(c == 0), stop=(c == DC - 1)),
             reads=[sqb_T, cx.ones_T], writes=[ssb_T])
    rstd_from_ss(cx, st, ssb, ssb_T, D)
    rstd = st.rstd
    for c in range(DC):
        P.op("dve", lambda e, c=c: e.scalar_tensor_tensor(
            out=xn[:, c, col0:col0 + 512], in0=xs[:, c, :], scalar=g_sb[:, c:c + 1],
            in1=rstd[:, :], op0=ALU.mult, op1=ALU.mult),
            reads=[xs_T, st.rstd_T, g_T], writes=[xn_T])


def load_small(cx, dram_ap, shape, dtype=F32, q="sp", name="c"):
    sb = cx.A.alloc(shape, dtype, name)
    t = T(name)
    idx = tuple(slice(None) for _ in shape)
    cx.P.dma(q, lambda e: [e.dma_start(out=sb[idx], in_=dram_ap)], writes=[t], semtile=t)
    return sb, t

def ffn_stage(cx, x_in, x_out, g_dram, w_in_r, w_out_r, DC, FC, NT, TT, eps=1e-6):
    nc, P, A = cx.nc, cx.P, cx.A
    D = DC * 128
    NS = TT // 512
    m0 = A.mark()
    st = NormScratch(cx, DC)
    xn = A.alloc([128, DC, TT], BF16, "xn"); xn_T = [T("xn%d" % s) for s in range(NS)]
    hT = A.alloc([128, FC, TT], BF16, "hT")
    hT_T = [[T("h") for s in range(NS)] for j in range(FC)]
    wi = Rot([(A.alloc([128, DC, 128], BF16, "wi"), T("wi")) for _ in range(4)])
    FH = (FC + 1) // 2
    wo = Rot([(A.alloc([128, FH, 128], BF16, "wo"), T("wo")) for _ in range(3)])
    xr = Rot([(A.alloc([128, 512], F32, "xr"), T("xr")) for _ in range(2)])
    xo = Rot([(A.alloc([128, 512], F32, "xo"), T("xo")) for _ in range(2)])
    sg = Rot([(A.alloc([128, 512], F32, "sg"), T("sg")) for _ in range(2)])
    g_sb = A.alloc([128, DC], F32, "g"); g_T = T("g")
    P.dma("sp", lambda e: [e.dma_start(out=g_sb[:, :], in_=g_dram)], writes=[g_T], semtile=g_T)
    ones = cx.ones_bf

    NTT = NT // TT

    def phase_a(tt, s, load=True):
        norm_subtile(cx, st, x_in, tt * TT + s * 512, g_sb, g_T, DC, xn, xn_T[s], s * 512, load=load)

    for s in range(NS):
        phase_a(0, s)
    for tt in range(NTT):
        for j in range(FC):
            banks = {}
            for kind in range(2):
                wb, wb_T = wi.next()
                P.dma("pool", lambda e, wb=wb, j=j, kind=kind: [e.dma_start(out=wb[:, :, :], in_=w_in_r[j * 2 + kind])],
                      writes=[wb_T], semtile=wb_T)
                for s in range(NS):
                    pb, pb_T = cx.ps.next()
                    banks[(kind, s)] = (pb, pb_T)
                    for kc in range(DC):
                        P.op("pe", lambda e, pb=pb, wb=wb, kc=kc, s=s: e.matmul(
                            pb[:, :], lhsT=wb[:, kc, :], rhs=xn[:, kc, s * 512:(s + 1) * 512],
                            start=(kc == 0), stop=(kc == DC - 1)),
                            reads=[wb_T, xn_T[s]], writes=[pb_T])
            for s in range(NS):
                gb, gb_T = banks[(0, s)]
                ub, ub_T = banks[(1, s)]
                sgb, sgb_T = sg.next()
                P.op("act", lambda e, gb=gb, sgb=sgb: e.activation(out=sgb[:, :], in_=gb[:, :], func=AF.Silu),
                     reads=[gb_T], writes=[sgb_T])
                P.op("dve", lambda e, ub=ub, sgb=sgb, j=j, s=s: e.tensor_tensor(
                    out=hT[:, j, s * 512:(s + 1) * 512], in0=ub[:, :], in1=sgb[:, :], op=ALU.mult),
                    reads=[ub_T, sgb_T], writes=[hT_T[j][s]])
        a_at = {(s + 1) * DC // (NS + 1): s for s in range(NS)} if tt + 1 < NTT else {}
        if a_at:
            norm_load(cx, st, x_in, (tt + 1) * TT)
        for i in range(DC):
            if i in a_at:
                phase_a(tt + 1, a_at[i], load=False)
                if a_at[i] + 1 < NS:
                    norm_load(cx, st, x_in, (tt + 1) * TT + (a_at[i] + 1) * 512)
            halves = []
            for hf in range(2):
                j0 = hf * FH
                j1 = min(FC, j0 + FH)
                wb, wb_T = wo.next()
                P.dma("pool", lambda e, wb=wb, i=i, j0=j0, j1=j1: [e.dma_start(out=wb[:, 0:j1 - j0, :],
                                                                           in_=w_out_r[i][:, j0:j1, :])],
                      writes=[wb_T], semtile=wb_T)
                halves.append((wb, wb_T, j0, j1))
            for s in range(NS):
                t0 = tt * TT + s * 512
                pb, pb_T = cx.ps.next()
                for (wb, wb_T, j0, j1) in halves:
                    for j in range(j0, j1):
                        P.op("pe", lambda e, pb=pb, wb=wb, j=j, j0=j0, s=s: e.matmul(
                            pb[:, :], lhsT=wb[:, j - j0, :], rhs=hT[:, j, s * 512:(s + 1) * 512],
                            start=(j == 0), stop=(j == FC - 1)),
                            reads=[wb_T, hT_T[j][s]], writes=[pb_T])
                xrb, xrb_T = xr.next()
                P.dma("sp", lambda e, xrb=xrb, i=i, t0=t0: [e.dma_start(out=xrb[:, :], in_=x_in[:, i, t0:t0 + 512])],
                      writes=[xrb_T], semtile=xrb_T)
                xob, xob_T = xo.next()
                P.op("dve", lambda e, pb=pb, xrb=xrb, xob=xob: e.scalar_tensor_tensor(
                    out=xob[:, :], in0=pb[:, :], scalar=0.5, in1=xrb[:, :], op0=ALU.mult, op1=ALU.add),
                    reads=[pb_T, xrb_T], writes=[xob_T])
                P.dma("sp", lambda e, xob=xob, i=i, t0=t0: [e.dma_start(out=x_out[:, i, t0:t0 + 512], in_=xob[:, :])],
                      reads=[xob_T], semtile=xob_T)
    A.reset(m0)


def eps_ap(cx):
    return cx.eps_sb[:, 0:1]


def mla_latent_stage(cx, x_in, g_dram, wd_r, wpe_r, gl_dram, cq_out, ckv_out, kpe_out, DC, NT):
    nc, P, A = cx.nc, cx.P, cx.A
    m0 = A.mark()
    st = NormScratch(cx, DC)
    g_sb, g_T = load_small(cx, g_dram, [128, DC], name="g")
    gl_sb, gl_T = load_small(cx, gl_dram, [128, 8], name="gl")
    wd = A.alloc([128, 8, DC, 128], BF16, "wd"); wd_T = [T("wd") for _ in range(8)]
    for m in range(8):
        P.dma("pool", lambda e, m=m: [e.dma_start(out=wd[:, m, :, :], in_=wd_r[m])],
              writes=[wd_T[m]], semtile=wd_T[m])
    wpe = A.alloc([128, 2, DC, 64], BF16, "wpe"); wpe_T = T("wpe")
    P.dma("pool", lambda e: [e.dma_start(out=wpe[:, 0, :, :], in_=wpe_r[0]),
                             e.dma_start(out=wpe[:, 1, :, :], in_=wpe_r[1])],
          writes=[wpe_T], semtile=wpe_T, n=2)
    xn = A.alloc([128, DC, 512], BF16, "xn"); xn_T = T("xn")
    stg = Rot([(A.alloc([128, 512], BF16, "stg"), T("stg")) for _ in range(4)])
    stf = Rot([(A.alloc([64, 512], F32, "stf"), T("stf")) for _ in range(2)])
    sq2 = Rot([(A.alloc([128, 512], BF16, "sq2"), T("sq2")) for _ in range(4)])
    tlr = Rot([(A.alloc([128, 512], F32, "tl"), T("tl")) for _ in range(8)])
    sts2 = []
    for _ in range(2):
        s2 = NormScratch.__new__(NormScratch)
        s2.rt = A.alloc([128, 512], F32, "rt2"); s2.rt_T = T("rt2")
        s2.rstd = s2.rt; s2.rstd_T = s2.rt_T
        sts2.append(s2)
    ones = cx.ones_bf
    for s in range(NT // 512):
        t0 = s * 512
        norm_subtile(cx, st, x_in, t0, g_sb, g_T, DC, xn, xn_T, 0)
        import os
        for part, outd in ((0, cq_out), (1, ckv_out)):
            if os.environ.get("LAT_SKIP") == "cq":
                break
            banks = []
            for mc in range(4):
                m = part * 4 + mc
                pb, pb_T = cx.ps.next()
                banks.append((pb, pb_T))
                for kc in range(DC):
                    P.op("pe", lambda e, pb=pb, m=m, kc=kc: e.matmul(
                        pb[:, :], lhsT=wd[:, m, kc, :], rhs=xn[:, kc, :],
                        start=(kc == 0), stop=(kc == DC - 1)),
                        reads=[wd_T[m], xn_T], writes=[pb_T])
            ssb, ssb_T = cx.ps.next()
            tls = []
            for mc in range(4):
                pb, pb_T = banks[mc]
                sqb, sqb_T = sq2.next()
                P.op("act", lambda e, pb=pb, sqb=sqb: e.activation(out=sqb[:, :], in_=pb[:, :], func=AF.Square),
                     reads=[pb_T], writes=[sqb_T])
                tl, tl_T = tlr.next()
                tls.append((tl, tl_T))
                P.op("act", lambda e, pb=pb, tl=tl, col=part * 4 + mc: e.activation(
                    out=tl[:, :], in_=pb[:, :], func=AF.Copy, scale=gl_sb[:, col:col + 1]),
                    reads=[pb_T, gl_T], writes=[tl_T])
                P.op("pe", lambda e, sqb=sqb, mc=mc, ssb=ssb: e.matmul(ssb[:, :], lhsT=ones[:, :], rhs=sqb[:, :],
                                                                        start=(mc == 0), stop=(mc == 3)),
                     reads=[sqb_T, cx.ones_T], writes=[ssb_T])
            stx = sts2[part]
            rstd_from_ss(cx, stx, ssb, ssb_T, 512)
            rstd = stx.rstd
            for mc in range(4):
                tl, tl_T = tls[mc]
                sb, sb_T = stg.next()
                P.op("dve", lambda e, tl=tl, sb=sb, rstd=rstd: e.tensor_tensor(
                    out=sb[:, :], in0=tl[:, :], in1=rstd[:, :], op=ALU.mult),
                    reads=[tl_T, stx.rstd_T], writes=[sb_T])
                P.dma("sp", lambda e, sb=sb, mc=mc, outd=outd, t0=t0: [e.dma_start(out=outd[:, mc, t0:t0 + 512], in_=sb[:, :])],
                      reads=[sb_T], semtile=sb_T)
        for ab in range(2):
            if os.environ.get("LAT_SKIP") == "kpe":
                break
            pb, pb_T = cx.ps.next()
            for kc in range(DC):
                P.op("pe", lambda e, pb=pb, ab=ab, kc=kc: e.matmul(
                    pb[0:64, :], lhsT=wpe[:, ab, kc, :], rhs=xn[:, kc, :],
                    start=(kc == 0), stop=(kc == DC - 1)),
                    reads=[wpe_T, xn_T], writes=[pb_T])
            sb, sb_T = stf.next()
            P.op("act", lambda e, pb=pb, sb=sb: e.activation(out=sb[:, :], in_=pb[0:64, :], func=AF.Copy),
                 reads=[pb_T], writes=[sb_T])
            P.dma("sp", lambda e, sb=sb, ab=ab, t0=t0: [e.dma_start(out=kpe_out[:, ab, t0:t0 + 512], in_=sb[:, :])],
                  reads=[sb_T], semtile=sb_T)
    A.reset(m0)


def proj_res_stage(cx, x_in, x_out, o_dram, w_r, DC, HC, NT, sel_dram=None):
    nc, P, A = cx.nc, cx.P, cx.A
    m0 = A.mark()
    o_sb = A.alloc([128, HC, NT], BF16, "o"); o_T = [T("o") for _ in range(NT // 512)]
    if sel_dram is None:
        for s in range(NT // 512):
            P.dma("sp", lambda e, s=s: [e.dma_start(out=o_sb[:, :, s * 512:(s + 1) * 512],
                                                    in_=o_dram[:, :, s * 512:(s + 1) * 512])],
                  writes=[o_T[s]], semtile=o_T[s])
    else:
        HS = HC // 2
        sel, sel_T = load_small(cx, sel_dram, [128, 2], name="sel")
        ca = Rot([(A.alloc([128, HS, 512], BF16, "ca"), T("ca")) for _ in range(2)])
        cb = Rot([(A.alloc([128, HS, 512], BF16, "cb"), T("cb")) for _ in range(2)])
        for s in range(NT // 512):
            for srcr in range(2):
                a, a_T = ca.next()
                b, b_T = cb.next()
                chunks = o_dram[srcr]
                P.dma("sp", lambda e, a=a, s=s, chunks=chunks: [e.dma_start(
                    out=a[:, h0:h0 + ap.shape[1], :], in_=ap[:, :, s * 512:(s + 1) * 512]) for (ap, h0) in chunks],
                    writes=[a_T], semtile=a_T, n=len(chunks))
                P.dma("sp", lambda e, b=b, s=s, chunks=chunks: [e.dma_start(
                    out=b[:, h0:h0 + ap.shape[1], :], in_=ap[:, :, NT + s * 512:NT + (s + 1) * 512]) for (ap, h0) in chunks],
                    writes=[b_T], semtile=b_T, n=len(chunks))
                P.op("dve", lambda e, a=a: e.tensor_scalar(out=a[:, :, :], in0=a[:, :, :], scalar1=sel[:, 0:1],
                                                           scalar2=None, op0=ALU.mult),
                     reads=[a_T, sel_T], writes=[a_T])
                P.op("dve", lambda e, a=a, b=b, s=s, srcr=srcr: e.scalar_tensor_tensor(
                    out=o_sb[:, srcr * HS:(srcr + 1) * HS, s * 512:(s + 1) * 512], in0=b[:, :, :],
                    scalar=sel[:, 1:2], in1=a[:, :, :], op0=ALU.mult, op1=ALU.add),
                    reads=[a_T, b_T, sel_T], writes=[o_T[s]])
    wr = Rot([(A.alloc([128, HC, 128], BF16, "wr"), T("wr")) for _ in range(4)])
    xo = Rot([(A.alloc([128, 512], F32, "xo"), T("xo")) for _ in range(3)])
    NSUBT = NT // 512
    items = [(i, s) for s in range(NSUBT) for i in range(DC)]
    LOOK = 2
    xr = Rot([(A.alloc([128, 512], F32, "xr"), T("xr")) for _ in range(LOOK + 2)])
    loaded = {}

    def issue_load(n):
        i, s = items[n]
        xrb, xrb_T = xr.next()
        P.dma("sp", lambda e: [e.dma_start(out=xrb[:, :], in_=x_in[:, i, s * 512:(s + 1) * 512])],
              writes=[xrb_T], semtile=xrb_T)
        loaded[n] = (xrb, xrb_T)

    def do_item(n, wb, wb_T):
        i, s = items[n]
        t0 = s * 512
        pb, pb_T = cx.ps.next()
        for hc in range(HC):
            P.op("pe", lambda e, hc=hc: e.matmul(pb[:, :], lhsT=wb[:, hc, :], rhs=o_sb[:, hc, t0:t0 + 512],
                                                 start=(hc == 0), stop=(hc == HC - 1)),
                 reads=[wb_T, o_T[s]], writes=[pb_T])
        if n + LOOK < len(items):
            issue_load(n + LOOK)
        xrb, xrb_T = loaded.pop(n)
        xob, xob_T = xo.next()
        P.op("dve", lambda e: e.tensor_tensor(out=xob[:, :], in0=pb[:, :], in1=xrb[:, :], op=ALU.add),
             reads=[pb_T, xrb_T], writes=[xob_T])
        P.dma("sp", lambda e: [e.dma_start(out=x_out[:, i, t0:t0 + 512], in_=xob[:, :])],
              reads=[xob_T], semtile=xob_T)

    for n in range(min(LOOK, len(items))):
        issue_load(n)
    for n, (i, s) in enumerate(items):
        wb, wb_T = wr.next()
        P.dma("pool", lambda e, wb=wb, i=i: [e.dma_start(out=wb[:, :, :], in_=w_r[i])],
              writes=[wb_T], semtile=wb_T)
        do_item(n, wb, wb_T)
    A.reset(m0)


def norm_out_stage(cx, x_in, g_dram, xn_out, DC, NT):
    nc, P, A = cx.nc, cx.P, cx.A
    m0 = A.mark()
    st = NormScratch(cx, DC)
    g_sb, g_T = load_small(cx, g_dram, [128, DC], name="g")
    xn = Rot([(A.alloc([128, DC, 512], BF16, "xn"), T("xn")) for _ in range(2)])
    for s in range(NT // 512):
        xb, xb_T = xn.next()
        norm_subtile(cx, st, x_in, s * 512, g_sb, g_T, DC, xb, xb_T, 0)
        nch = len(xn_out)
        cw = DC // nch
        P.dma("sp", lambda e, xb=xb, s=s: [e.dma_start(out=xn_out[k][:, :, s * 512:(s + 1) * 512],
                                                       in_=xb[:, k * cw:(k + 1) * cw, :]) for k in range(nch)],
              reads=[xb_T], semtile=xb_T, n=nch)
    A.reset(m0)


def mla_attn_stage(cx, cq_f, ckv_f, kpe_f, rope_t, wuq_r, wuk_r, wuv_r, gv_dram, tri_dram, o_out,
                   NH, S, QK=192):
    nc, P, A = cx.nc, cx.P, cx.A
    m0 = A.mark()
    NQ = S // 512
    scale = 1.0 / float(np.sqrt(QK))
    ones = cx.ones_bf
    sts = []
    for _ in range(2):
        st = NormScratch.__new__(NormScratch)
        st.rt = A.alloc([128, 512], F32, "rt"); st.rt_T = T("rt")
        st.rstd = st.rt; st.rstd_T = st.rt_T
        sts.append(st)
    gv, gv_T = load_small(cx, gv_dram, [128, 6], name="gv")
    tri = A.alloc([128, 128], BF16, "tri"); tri_T = T("tri")
    P.dma("pool", lambda e: [e.dma_start(out=tri[:, :], in_=tri_dram)], writes=[tri_T], semtile=tri_T)
    cq = A.alloc([128, 4, S], BF16, "cq"); cq_T = T("cq")
    ckv = A.alloc([128, 4, S], BF16, "ckv"); ckv_T = T("ckv")
    H2 = S // 2
    P.dma("sp", lambda e: [e.dma_start(out=cq[:, :, hh * H2:(hh + 1) * H2], in_=cq_f[hh]) for hh in range(2)],
          writes=[cq_T], semtile=cq_T, n=2)
    P.dma("sp", lambda e: [e.dma_start(out=ckv[:, :, hh * H2:(hh + 1) * H2], in_=ckv_f[hh]) for hh in range(2)],
          writes=[ckv_T], semtile=ckv_T, n=2)
    kR = A.alloc([64, S], F32, "kR"); kR_T = T("kR")
    sqpe = A.alloc([64, S], BF16, "sqpe"); sqpe_T = T("sqpe")
    m1 = A.mark()
    kab = A.alloc([64, 2, S], F32, "kab"); kab_T = T("kab")
    tab = A.alloc([64, 2, S], F32, "tab"); tab_T = T("tab")
    tmp = A.alloc([64, S], F32, "tmp"); tmp_T = T("tmp")
    P.dma("sp", lambda e: [e.dma_start(out=kab[:, :, hh * H2:(hh + 1) * H2], in_=kpe_f[hh]) for hh in range(2)],
          writes=[kab_T], semtile=kab_T, n=2)
    P.dma("sp", lambda e: [e.dma_start(out=tab[:, :, :], in_=rope_t)], writes=[tab_T], semtile=tab_T)
    P.op("act", lambda e: e.activation(out=sqpe[:, :], in_=kab[:, 0, :], func=AF.Square),
         reads=[kab_T], writes=[sqpe_T])
    P.op("dve", lambda e: e.scalar_tensor_tensor(out=kR[:, :], in0=kab[:, 0, :], scalar=gv[0:64, 4:5],
                                                 in1=tab[:, 0, :], op0=ALU.mult, op1=ALU.mult),
         reads=[kab_T, tab_T, gv_T], writes=[kR_T])
    P.op("dve", lambda e: e.scalar_tensor_tensor(out=tmp[:, :], in0=kab[:, 1, :], scalar=gv[0:64, 5:6],
                                                 in1=tab[:, 1, :], op0=ALU.mult, op1=ALU.mult),
         reads=[kab_T, tab_T, gv_T], writes=[tmp_T])
    P.op("dve", lambda e: e.tensor_tensor(out=kR[:, :], in0=kR[:, :], in1=tmp[:, :], op=ALU.add),
         reads=[tmp_T, kR_T], writes=[kR_T])
    P.fence()
    A.reset(m1)
    V4 = A.alloc([128, S // 128, 512], BF16, "V4"); V4_T = [T("V4") for _ in range(S // 128)]
    kn = A.alloc([128, S], BF16, "kn"); kn_T = [T("kn") for _ in range(NQ)]
    kpn = A.alloc([64, S], BF16, "kpn"); kpn_T = [T("kpn") for _ in range(NQ)]
    qn = A.alloc([128, S], BF16, "qn"); qn_T = [T("qn") for _ in range(NQ)]
    qpn = A.alloc([64, S], BF16, "qpn"); qpn_T = [T("qpn") for _ in range(NQ)]
    wuv = Rot([(A.alloc([128, 4, 512], BF16, "wuv"), T("wuv")) for _ in range(1)])
    wuq = Rot([(A.alloc([128, 4, 256], BF16, "wuq"), T("wuq")) for _ in range(2)])
    wuk = Rot([(A.alloc([128, 4, 128], BF16, "wuk"), T("wuk")) for _ in range(2)])
    sq = Rot([(A.alloc([128, 512], BF16, "sq"), T("sq")) for _ in range(2)])
    sq2 = Rot([(A.alloc([64, 512], BF16, "sq2"), T("sq2")) for _ in range(2)])
    rtab = Rot([(A.alloc([64, 2, 512], F32, "rtab"), T("rtab")) for _ in range(2)])
    r1 = Rot([(A.alloc([64, 512], F32, "r1"), T("r1")) for _ in range(2)])
    r2 = Rot([(A.alloc([64, 512], F32, "r2"), T("r2")) for _ in range(2)])
    tqr = Rot([(A.alloc([128, 512], F32, "tq"), T("tq")) for _ in range(2)])
    pT = Rot([(A.alloc([128, 512], BF16, "pT"), T("pT")) for _ in range(4)])
    rc = Rot([(A.alloc([128, 512], F32, "rc"), T("rc")) for _ in range(2)])
    ost = Rot([(A.alloc([128, 512], BF16, "ost"), T("ost")) for _ in range(2)])
    rot = PsRot(cx.banks[0:4])
    rot8 = PsRot(cx.banks)
    acc = Rot([(cx.banks[4], cx.banks[5]), (cx.banks[6], cx.banks[7])])

    for hg in range(NH // 4):
        wvb, wvb_T = wuv.next()
        P.dma("pool", lambda e, wvb=wvb, hg=hg: [e.dma_start(out=wvb[:, :, :], in_=wuv_r[hg])],
              writes=[wvb_T], semtile=wvb_T)
        for tc in range(S // 128):
            pb, pb_T = rot.next()
            for c in range(4):
                P.op("pe", lambda e, pb=pb, wvb=wvb, c=c, tc=tc: e.matmul(
                    pb[:, :], lhsT=ckv[:, c, tc * 128:(tc + 1) * 128], rhs=wvb[:, c, :],
                    start=(c == 0), stop=(c == 3)), reads=[ckv_T, wvb_T], writes=[pb_T])
            if tc % 2 == 0:
                P.op("act", lambda e, pb=pb, tc=tc: e.activation(out=V4[:, tc, :], in_=pb[:, :], func=AF.Copy),
                     reads=[pb_T], writes=[V4_T[tc]])
            else:
                P.op("dve", lambda e, pb=pb, tc=tc: e.tensor_copy(out=V4[:, tc, :], in_=pb[:, :]),
                     reads=[pb_T], writes=[V4_T[tc]])
        def do_head(hh, lh, wqb, wqb_T, wkb, wkb_T):
            P.dma("pool", lambda e, wqb=wqb, lh=lh: [e.dma_start(out=wqb[:, :, :], in_=wuq_r[lh])],
                  writes=[wqb_T], semtile=wqb_T)
            P.dma("pool", lambda e, wkb=wkb, lh=lh: [e.dma_start(out=wkb[:, :, :], in_=wuk_r[lh])],
                  writes=[wkb_T], semtile=wkb_T)
            def k_proj(qt):
                c0, c1 = qt * 512, (qt + 1) * 512
                pb, pb_T = rot8.next()
                for c in range(4):
                    P.op("pe", lambda e, c=c: e.matmul(
                        pb[:, :], lhsT=wkb[:, c, :], rhs=ckv[:, c, c0:c1], start=(c == 0), stop=(c == 3)),
                        reads=[wkb_T, ckv_T], writes=[pb_T])
                sqb, sqb_T = sq.next()
                P.op("act", lambda e: e.activation(out=sqb[:, :], in_=pb[:, :], func=AF.Square),
                     reads=[pb_T], writes=[sqb_T])
                tq, tq_T = tqr.next()
                P.op("act", lambda e: e.activation(out=tq[:, :], in_=pb[:, :], func=AF.Copy, scale=gv[:, 3:4]),
                     reads=[pb_T, gv_T], writes=[tq_T])
                return (sqb, sqb_T, tq, tq_T)

            def k_norm(qt, sqb, sqb_T, tq, tq_T):
                c0, c1 = qt * 512, (qt + 1) * 512
                stx = sts[qt % 2]
                rstd_l = stx.rstd
                ssb, ssb_T = rot8.next()
                P.op("pe", lambda e: e.matmul(ssb[:, :], lhsT=ones[:, :], rhs=sqb[:, :], start=True, stop=False),
                     reads=[sqb_T, cx.ones_T], writes=[ssb_T])
                P.op("pe", lambda e: e.matmul(ssb[:, :], lhsT=ones[0:64, :], rhs=sqpe[0:64, c0:c1],
                                              start=False, stop=True),
                     reads=[sqpe_T, cx.ones_T], writes=[ssb_T])
                rstd_from_ss(cx, stx, ssb, ssb_T, QK)
                P.op("dve", lambda e: e.tensor_tensor(out=kn[:, c0:c1], in0=tq[:, :], in1=rstd_l[:, :], op=ALU.mult),
                     reads=[tq_T, stx.rstd_T], writes=[kn_T[qt]])
                P.op("pool", lambda e: e.tensor_tensor(
                    out=kpn[:, c0:c1], in0=kR[:, c0:c1], in1=rstd_l[0:64, :], op=ALU.mult),
                    reads=[kR_T, stx.rstd_T], writes=[kpn_T[qt]])

            pend = k_proj(0)
            for qt in range(NQ):
                nxt = k_proj(qt + 1) if qt + 1 < NQ else None
                k_norm(qt, *pend)
                pend = nxt

            def q_proj(qt):
                c0, c1 = qt * 512, (qt + 1) * 512
                bq, bq_T = rot8.next()
                bA, bA_T = rot8.next()
                bB, bB_T = rot8.next()
                for (pb, pb_T, lo, hi, rows) in ((bq, bq_T, 0, 128, 128), (bA, bA_T, 128, 192, 64), (bB, bB_T, 192, 256, 64)):
                    for c in range(4):
                        P.op("pe", lambda e, pb=pb, c=c, lo=lo, hi=hi, rows=rows: e.matmul(
                            pb[0:rows, :], lhsT=wqb[:, c, lo:hi], rhs=cq[:, c, c0:c1], start=(c == 0), stop=(c == 3)),
                            reads=[wqb_T, cq_T], writes=[pb_T])
                sqb, sqb_T = sq.next()
                sqc, sqc_T = sq2.next()
                P.op("act", lambda e: e.activation(out=sqb[:, :], in_=bq[:, :], func=AF.Square),
                     reads=[bq_T], writes=[sqb_T])
                P.op("act", lambda e: e.activation(out=sqc[:, :], in_=bA[0:64, :], func=AF.Square),
                     reads=[bA_T], writes=[sqc_T])
                tq, tq_T = tqr.next()
                P.op("act", lambda e: e.activation(out=tq[:, :], in_=bq[:, :], func=AF.Copy, scale=gv[:, 0:1]),
                     reads=[bq_T, gv_T], writes=[tq_T])
                tb, tb_T = rtab.next()
                P.dma("sp", lambda e: [e.dma_start(out=tb[:, :, :], in_=rope_t[:, :, c0:c1])],
                      writes=[tb_T], semtile=tb_T)
                a1, a1_T = r1.next()
                a2, a2_T = r2.next()
                P.op("dve", lambda e: e.scalar_tensor_tensor(
                    out=a1[:, :], in0=bA[0:64, :], scalar=gv[0:64, 1:2], in1=tb[:, 0, :],
                    op0=ALU.mult, op1=ALU.mult), reads=[bA_T, gv_T, tb_T], writes=[a1_T])
                P.op("dve", lambda e: e.scalar_tensor_tensor(
                    out=a2[:, :], in0=bB[0:64, :], scalar=gv[0:64, 2:3], in1=tb[:, 1, :],
                    op0=ALU.mult, op1=ALU.mult), reads=[bB_T, gv_T, tb_T], writes=[a2_T])
                P.op("pool", lambda e: e.tensor_tensor(out=a1[:, :], in0=a1[:, :], in1=a2[:, :], op=ALU.add),
                     reads=[a1_T, a2_T], writes=[a1_T])
                return (sqb, sqb_T, sqc, sqc_T, tq, tq_T, a1, a1_T)

            def q_norm(qt, sqb, sqb_T, sqc, sqc_T, tq, tq_T, a1, a1_T):
                c0, c1 = qt * 512, (qt + 1) * 512
                stx = sts[qt % 2]
                rstd_l = stx.rstd
                ssb, ssb_T = rot8.next()
                P.op("pe", lambda e: e.matmul(ssb[:, :], lhsT=ones[:, :], rhs=sqb[:, :], start=True, stop=False),
                     reads=[sqb_T, cx.ones_T], writes=[ssb_T])
                P.op("pe", lambda e: e.matmul(ssb[:, :], lhsT=ones[0:64, :], rhs=sqc[0:64, :], start=False, stop=True),
                     reads=[sqc_T, cx.ones_T], writes=[ssb_T])
                rstd_from_ss(cx, stx, ssb, ssb_T, QK)
                P.op("dve", lambda e: e.tensor_tensor(out=qn[:, c0:c1], in0=tq[:, :], in1=rstd_l[:, :], op=ALU.mult),
                     reads=[tq_T, stx.rstd_T], writes=[qn_T[qt]])
                P.op("pool", lambda e: e.tensor_tensor(
                    out=qpn[:, c0:c1], in0=a1[:, :], in1=rstd_l[0:64, :], op=ALU.mult),
                    reads=[a1_T, stx.rstd_T], writes=[qpn_T[qt]])

            pend = q_proj(0)
            for qt in range(NQ):
                nxt = q_proj(qt + 1) if qt + 1 < NQ else None
                q_norm(qt, *pend)
                pend = nxt

            items = [(qt, kc) for qt in range(NQ) for kc in range(4 * qt + 4)]
            accs = {}

            def a_qk(qt, kc):
                dk = kc - 4 * qt
                lo = max(0, dk) * 128
                k0, k1 = kc * 128, (kc + 1) * 128
                q0, q1 = qt * 512 + lo, (qt + 1) * 512
                sb_, sb_T = rot.next()
                P.op("pe", lambda e: e.matmul(sb_[:, lo:512], lhsT=kn[:, k0:k1], rhs=qn[:, q0:q1], start=True, stop=False),
                     reads=[kn_T[kc // 4], qn_T[qt]], writes=[sb_T])
                P.op("pe", lambda e: e.matmul(sb_[:, lo:512], lhsT=kpn[0:64, k0:k1], rhs=qpn[0:64, q0:q1],
                                              start=False, stop=True),
                     reads=[kpn_T[kc // 4], qpn_T[qt]], writes=[sb_T])
                pb, pb_T = pT.next()
                P.op("act", lambda e: e.activation(out=pb[:, lo:512], in_=sb_[:, lo:512], func=AF.Exp, scale=scale),
                     reads=[sb_T], writes=[pb_T])
                if dk >= 0:
                    P.op("pool", lambda e: e.tensor_tensor(
                        out=pb[:, lo:lo + 128], in0=pb[:, lo:lo + 128], in1=tri[:, :], op=ALU.mult),
                        reads=[pb_T, tri_T], writes=[pb_T])
                return (pb, pb_T, lo)

            def a_pv(qt, kc, pb, pb_T, lo):
                nk = 4 * qt + 4
                if kc == 0:
                    accs[qt] = acc.next()
                (ob, ob_T), (db, db_T) = accs[qt]
                P.op("pe", lambda e: e.matmul(ob[:, lo:512], lhsT=V4[:, kc, hh * 128:(hh + 1) * 128], rhs=pb[:, lo:512],
                                              start=(kc == 0), stop=(kc == nk - 1)),
                     reads=[V4_T[kc], pb_T], writes=[ob_T])
                P.op("pe", lambda e: e.matmul(db[:, lo:512], lhsT=ones[:, :], rhs=pb[:, lo:512],
                                              start=(kc == 0), stop=(kc == nk - 1)),
                     reads=[cx.ones_T, pb_T], writes=[db_T])
                if kc == nk - 1:
                    rcb, rcb_T = rc.next()
                    P.op("dve", lambda e: e.reciprocal(out=rcb[:, :], in_=db[:, :]), reads=[db_T], writes=[rcb_T])
                    osb, osb_T = ost.next()
                    P.op("dve", lambda e: e.tensor_tensor(out=osb[:, :], in0=ob[:, :], in1=rcb[:, :], op=ALU.mult),
                         reads=[ob_T, rcb_T], writes=[osb_T])
                    P.dma("sp", lambda e: [e.dma_start(out=o_out(lh)[:, qt * 512:(qt + 1) * 512], in_=osb[:, :])],
                          reads=[osb_T], semtile=osb_T)

            LOOK = 2
            inflight = []
            for n, (qt, kc) in enumerate(items):
                inflight.append((qt, kc) + a_qk(qt, kc))
                if len(inflight) > LOOK:
                    a_pv(*inflight.pop(0))
            while inflight:
                a_pv(*inflight.pop(0))
        for hh in range(4):
            wqb, wqb_T = wuq.next()
            wkb, wkb_T = wuk.next()
            do_head(hh, hg * 4 + hh, wqb, wqb_T, wkb, wkb_T)
    A.reset(m0)


DIL_D = (1, 4, 16)


def dil_attn_stage(cx, xn_f, w_r, gv_dram, bt_dram, ident_dram, o_out, NHS, S, DC):
    nc, P, A = cx.nc, cx.P, cx.A
    m0 = A.mark()
    DH = 128
    scale = 1.0 / float(np.sqrt(DH))
    ones = cx.ones_bf
    NSUB = S // 512
    st = NormScratch.__new__(NormScratch)
    st.rt = A.alloc([128, 512], F32, "rt"); st.rt_T = T("rt")
    st.rstd = A.alloc([128, 512], F32, "rstd"); st.rstd_T = T("rstd")
    gv, gv_T = load_small(cx, gv_dram, [128, 2], name="gv")
    ident = A.alloc([128, 128], BF16, "ident"); ident_T = T("ident")
    P.dma("pool", lambda e: [e.dma_start(out=ident[:, :], in_=ident_dram)], writes=[ident_T], semtile=ident_T)
    bt = A.alloc([128, NHS * 3, 2, 256], BF16, "bt"); bt_T = T("bt")
    P.dma("pool", lambda e: [e.dma_start(out=bt[:, i, :, :], in_=bt_dram[i]) for i in range(NHS * 3)],
          writes=[bt_T], semtile=bt_T, n=NHS * 3)
    wts = Rot([(A.alloc([128, DC, 128], BF16, "w"), T("w")) for _ in range(10)])
    qs = [A.alloc([128, DIL_D[g], S // DIL_D[g]], BF16, "qs%d" % g) for g in range(3)]
    ks = [A.alloc([128, DIL_D[g], S // DIL_D[g]], BF16, "ks%d" % g) for g in range(3)]
    vs = [A.alloc([128, DIL_D[g], S // DIL_D[g]], BF16, "vs%d" % g) for g in range(3)]
    qkv_T = [[T("qkv") for _ in range(3)] for g in range(3)]
    sq = Rot([(A.alloc([128, 512], BF16, "sq"), T("sq")) for _ in range(2)])
    m1 = A.mark()
    trb, trb_T = cx.banks[7]
    trv = trb[:, :].bitcast(BF16)[:, 0:512].rearrange("p (b e) -> p b e", b=4)
    rstd = st.rstd

    rotP = PsRot(cx.banks[0:7])
    rotA = PsRot(cx.banks[0:4])
    accA = Rot([(cx.banks[4], cx.banks[5]), (cx.banks[6], cx.banks[7])])

    def load_weights(hs):
        wl = []
        for gk in range(9):
            wb, wb_T = wts.next()
            P.dma("pool", lambda e, wb=wb, gk=gk: [e.dma_start(out=wb[:, :, :], in_=w_r[hs, gk])],
                  writes=[wb_T], semtile=wb_T)
            wl.append((wb, wb_T))
        return wl

    def do_slot(hs, wl):
        A.reset(m1)
        xnr = Rot([(A.alloc([128, DC, 512], BF16, "xnr"), T("xnr")) for _ in range(2)])

        def load_x(s):
            xb, xb_T = xnr.next()
            srcs = xn_f(s)
            cw = DC // len(srcs)
            P.dma("sp", lambda e: [e.dma_start(out=xb[:, k * cw:(k + 1) * cw, :], in_=srcs[k]) for k in range(len(srcs))],
                  writes=[xb_T], semtile=xb_T, n=len(srcs))
            return xb, xb_T

        def p_proj(s, g, kind, xb, xb_T):
            d = DIL_D[g]
            l0, lw = s * 512 // d, 512 // d
            wb, wb_T = wl[g * 3 + kind]
            pb, pb_T = rotP.next()
            for kc in range(DC):
                P.op("pe", lambda e, kc=kc: e.matmul(pb[:, :], lhsT=wb[:, kc, :], rhs=xb[:, kc, :],
                                                     start=(kc == 0), stop=(kc == DC - 1)),
                     reads=[wb_T, xb_T], writes=[pb_T])
            dst = (qs, ks, vs)[kind][g]
            if d == 1:
                src_v = pb[:, :]
                dst_v = dst[:, 0, l0:l0 + lw]
            else:
                src_v = pb[:, :].rearrange("p (l r) -> p r l", r=d)
                dst_v = dst[:, :, l0:l0 + lw]
            if kind == 2:
                P.op("act", lambda e: e.activation(out=dst_v, in_=src_v, func=AF.Copy),
                     reads=[pb_T], writes=[qkv_T[g][kind]])
                return None
            sqb, sqb_T = sq.next()
            P.op("act", lambda e: e.activation(out=sqb[:, :], in_=pb[:, :], func=AF.Square),
                 reads=[pb_T], writes=[sqb_T])
            return (g, kind, pb, pb_T, sqb, sqb_T, src_v, dst_v)

        def p_norm(g, kind, pb, pb_T, sqb, sqb_T, src_v, dst_v):
            d = DIL_D[g]
            ssb, ssb_T = rotP.next()
            P.op("pe", lambda e: e.matmul(ssb[:, :], lhsT=ones[:, :], rhs=sqb[:, :], start=True, stop=True),
                 reads=[sqb_T, cx.ones_T], writes=[ssb_T])
            rstd_from_ss(cx, st, ssb, ssb_T, DH)
            rs_v = rstd[:, :] if d == 1 else rstd[:, :].rearrange("p (l r) -> p r l", r=d)
            P.op("dve", lambda e: e.scalar_tensor_tensor(
                out=dst_v, in0=src_v, scalar=gv[:, kind:kind + 1], in1=rs_v, op0=ALU.mult, op1=ALU.mult),
                reads=[pb_T, gv_T, st.rstd_T], writes=[qkv_T[g][kind]])

        pend = None
        for s in range(NSUB):
            xb, xb_T = load_x(s)
            for g in range(3):
                for kind in range(3):
                    cur = p_proj(s, g, kind, xb, xb_T)
                    if pend is not None:
                        p_norm(*pend)
                    pend = cur
        if pend is not None:
            p_norm(*pend)
        P.fence()
        wl_next = load_weights(hs + 1) if hs + 1 < NHS else None
        A.reset(m1)
        Vt = [A.alloc([128, S // DIL_D[g] // 128 * DIL_D[g], 128], BF16, "Vt%d" % g) for g in range(3)]
        Vt_T = [T("Vt") for g in range(3)]
        oacc = A.alloc([128, S], F32, "oacc"); oacc_T = T("oacc")
        dacc = A.alloc([128, S], F32, "dacc"); dacc_T = T("dacc")
        pT = Rot([(A.alloc([128, 256], BF16, "pT"), T("pT")) for _ in range(4)])
        ost = Rot([(A.alloc([128, 512], BF16, "ost"), T("ost")) for _ in range(2)])
        for g in range(3):
            d = DIL_D[g]
            nb = S // d // 128
            nblk = d * nb
            for b0 in range(0, nblk, 4):
                for bi in range(4):
                    blk = b0 + bi
                    r, m = blk // nb, blk % nb
                    P.op("pe", lambda e, g=g, r=r, m=m, bi=bi: e.transpose(
                        trv[:, bi, :], vs[g][:, r, m * 128:(m + 1) * 128], ident[:, :]),
                        reads=[qkv_T[g][2], ident_T], writes=[trb_T])
                if (b0 // 4) % 2 == 0:
                    P.op("act", lambda e, g=g, b0=b0: e.activation(out=Vt[g][:, b0:b0 + 4, :], in_=trv[:, :, :], func=AF.Copy),
                         reads=[trb_T], writes=[Vt_T[g]])
                else:
                    P.op("dve", lambda e, g=g, b0=b0: e.tensor_copy(out=Vt[g][:, b0:b0 + 4, :], in_=trv[:, :, :]),
                         reads=[trb_T], writes=[Vt_T[g]])
        items = []
        for g in range(3):
            d = DIL_D[g]
            nb = S // d // 128
            QB = min(4, nb)
            for r in range(d):
                for mq0 in range(0, nb, QB):
                    mlist = [m for m in range(mq0 - 1, mq0 + QB) if m >= 0]
                    for m in mlist:
                        items.append((g, r, mq0, m, m == mlist[0], m == mlist[-1]))
        accs = {}

        def a_qk(g, r, mq0, m, first, last):
            d = DIL_D[g]
            nb = S // d // 128
            QB = min(4, nb)
            bidx = hs * 3 + g
            qlo = max(m, mq0)
            qhi = min(m + 1, mq0 + QB - 1)
            w = (qhi - qlo + 1) * 128
            tc0 = (qlo - mq0) * 128
            bc0 = (qlo - m) * 128
            sb_, sb_T = rotA.next()
            P.op("pe", lambda e: e.matmul(sb_[:, 0:w], lhsT=ks[g][:, r, m * 128:(m + 1) * 128],
                                          rhs=qs[g][:, r, qlo * 128:qlo * 128 + w], start=True, stop=False),
                 reads=[qkv_T[g][0], qkv_T[g][1]], writes=[sb_T])
            for hl in range(2):
                P.op("pe", lambda e, hl=hl: e.matmul(sb_[:, 0:w], lhsT=ident[:, :], rhs=bt[:, bidx, hl, bc0:bc0 + w],
                                                     start=False, stop=(hl == 1)),
                     reads=[ident_T, bt_T], writes=[sb_T])
            pb, pb_T = pT.next()
            P.op("act", lambda e: e.activation(out=pb[:, 0:w], in_=sb_[:, 0:w], func=AF.Exp, scale=scale),
                 reads=[sb_T], writes=[pb_T])
            return (pb, pb_T, w, tc0)

        def a_pv(g, r, mq0, m, first, last, pb, pb_T, w, tc0):
            d = DIL_D[g]
            nb = S // d // 128
            QB = min(4, nb)
            W = QB * 128
            if first:
                accs[(g, r, mq0)] = accA.next()
            (ao, ao_T), (ad, ad_T) = accs[(g, r, mq0)]
            blk = r * nb + m
            P.op("pe", lambda e: e.matmul(ao[:, tc0:tc0 + w], lhsT=Vt[g][:, blk, :], rhs=pb[:, 0:w],
                                          start=first, stop=last, skip_group_check=True),
                 reads=[Vt_T[g], pb_T], writes=[ao_T])
            P.op("pe", lambda e: e.matmul(ad[:, tc0:tc0 + w], lhsT=ones[:, :], rhs=pb[:, 0:w],
                                          start=first, stop=last, skip_group_check=True),
                 reads=[cx.ones_T, pb_T], writes=[ad_T])
            if not last:
                return
            l0 = mq0 * 128
            if d == 1:
                ov = oacc[:, l0:l0 + W]
                dv = dacc[:, l0:l0 + W]
            else:
                ov = oacc[:, :].rearrange("p (l r) -> p r l", r=d)[:, r, l0:l0 + W]
                dv = dacc[:, :].rearrange("p (l r) -> p r l", r=d)[:, r, l0:l0 + W]
            if g == 0:
                P.op("act", lambda e: e.activation(out=ov, in_=ao[:, 0:W], func=AF.Copy),
                     reads=[ao_T], writes=[oacc_T])
                P.op("dve", lambda e: e.tensor_copy(out=dv, in_=ad[:, 0:W]), reads=[ad_T], writes=[dacc_T])
            else:
                P.op("dve", lambda e: e.tensor_tensor(out=ov, in0=ao[:, 0:W], in1=ov, op=ALU.add),
                     reads=[ao_T, oacc_T], writes=[oacc_T])
                P.op("dve", lambda e: e.tensor_tensor(out=dv, in0=ad[:, 0:W], in1=dv, op=ALU.add),
                     reads=[ad_T, dacc_T], writes=[dacc_T])

        LOOK = 2
        inflight = []
        for it in items:
            inflight.append(it + a_qk(*it))
            if len(inflight) > LOOK:
                a_pv(*inflight.pop(0))
        while inflight:
            a_pv(*inflight.pop(0))
        P.op("dve", lambda e: e.reciprocal(out=dacc[:, :], in_=dacc[:, :]), reads=[dacc_T], writes=[dacc_T])
        for s in range(NSUB):
            osb, osb_T = ost.next()
            P.op("dve", lambda e, osb=osb, s=s: e.tensor_tensor(
                out=osb[:, :], in0=oacc[:, s * 512:(s + 1) * 512], in1=dacc[:, s * 512:(s + 1) * 512], op=ALU.mult),
                reads=[oacc_T, dacc_T], writes=[osb_T])
            P.dma("sp", lambda e, osb=osb, s=s: [e.dma_start(out=o_out(hs)[:, s * 512:(s + 1) * 512], in_=osb[:, :])],
                  reads=[osb_T], semtile=osb_T)
        P.fence()
        return wl_next

    wl = load_weights(0)
    for hs in range(NHS):
        wl = do_slot(hs, wl)
    A.reset(m0)


D_MODEL, SEQ, BATCH, DFF = 2048, 4096, 4, 5632
DC_, FC_ = D_MODEL // 128, DFF // 128
NCORE = 8
NT_ = SEQ // 2
TT_ = 1024


def lay_x(x):
    NT, D = x.shape
    return np.ascontiguousarray(x.T.reshape(D // 128, 128, NT).transpose(1, 0, 2))


def unlay_x(xl):
    p, DC, NT = xl.shape
    return np.ascontiguousarray(xl.transpose(1, 0, 2).reshape(DC * 128, NT).T)


def lay_vec(g):
    return np.ascontiguousarray(g.reshape(-1, 128).T)


def lay_win(w_in):
    D, F2 = w_in.shape
    DC, FC = D // 128, F2 // 256
    w = w_in.reshape(DC, 128, 2, FC, 128)
    return np.ascontiguousarray(w.transpose(3, 2, 1, 0, 4).reshape(FC * 2, 128, DC, 128))


def lay_wout(w_out):
    K, D = w_out.shape
    w = w_out.reshape(K // 128, 128, D // 128, 128)
    return np.ascontiguousarray(w.transpose(2, 1, 0, 3))


def lay_cols(w, cols):
    K = w.shape[0]
    return np.ascontiguousarray(w[:, cols].reshape(K // 128, 128, len(cols)).transpose(1, 0, 2))


def _dram(nc, name, shape, dt, kind):
    return nc.dram_tensor(name, list(shape), dt, kind=kind).ap()


def _run(nc, in_maps):
    res = run_bass_kernel_spmd(nc, in_maps, core_ids=list(range(NCORE)))
    return res.results


_GROUPS = [[0, 1], [2, 3], [4, 5], [6, 7]]


def _allgather(cx, src_t, dst_t):
    t = T("cc")
    cx.P.cc(lambda e: [e.collective_compute("AllGather", ALU.bypass, replica_groups=_GROUPS,
                                            ins=[src_t.ap().opt()], outs=[dst_t.ap().opt()])],
            semtile=t)


def build_fused():
    nc = bass.Bass("TRN2", target_bir_lowering=False)
    I = lambda name, shape, dt=F32: _dram(nc, name, shape, dt, "ExternalInput")
    x_in = I("x_in", [128, DC_, NT_])
    gs = [I("g%d" % k, [128, DC_]) for k in range(6)]
    wins = [I("win%d" % k, [2 * FC_, 128, DC_, 128]) for k in range(4)]
    wouts = [I("wout%d" % k, [DC_, 128, FC_, 128]) for k in range(4)]
    wd_d = I("wd", [8, 128, DC_, 128]); wpe_d = I("wpe", [2, 128, DC_, 64]); gl_d = I("gl", [128, 8])
    rope_d = I("rope_t", [64, 2, SEQ])
    wuq_d = I("wuq", [8, 128, 4, 256]); wuk_d = I("wuk", [8, 128, 4, 128]); wuv_d = I("wuv", [2, 128, 4, 512])
    gv_d = I("gv", [128, 6]); tri_d = I("tri", [128, 128])
    womla_d = I("womla", [DC_, 128, 16, 128]); sel_d = I("sel", [128, 2])
    wdil_d = I("wdil", [4, 9, 128, DC_, 128]); gv2_d = I("gv2", [128, 2]); bt_d = I("bt", [12, 128, 2, 256])
    id_d = I("ident", [128, 128]); wodil_d = I("wodil", [DC_, 128, 8, 128])
    x6 = _dram(nc, "x6", [128, DC_, NT_], F32, "ExternalOutput")
    N = lambda name, shape, dt: nc.dram_tensor(name, list(shape), dt)
    xs_ = [_dram(nc, "xs%d" % k, [128, DC_, NT_], F32, "Internal") for k in range(5)]
    H2 = NT_
    cq_l = N("cq_l", [512, H2], BF16); ckv_l = N("ckv_l", [512, H2], BF16); kpe_l = N("kpe_l", [128, H2], F32)
    cq_ag = N("cq_ag", [1024, H2], BF16); ckv_ag = N("ckv_ag", [1024, H2], BF16); kpe_ag = N("kpe_ag", [256, H2], F32)
    o_l = [N("o_l%d" % k, [256, SEQ], BF16) for k in range(4)]; o_ag = [N("o_ag%d" % k, [512, SEQ], BF16) for k in range(4)]
    xn_l = [N("xn_l%d" % k, [512, H2], BF16) for k in range(4)]; xn_ag = [N("xn_ag%d" % k, [1024, H2], BF16) for k in range(4)]
    o2_l = [N("o2_l%d" % k, [256, SEQ], BF16) for k in range(2)]; o2_ag = [N("o2_ag%d" % k, [512, SEQ], BF16) for k in range(2)]
    cx = Ctx(nc)
    P = cx.P
    x1, x2, x3, x4, x5 = xs_
    ffn_stage(cx, x_in, x1, gs[0], wins[0], wouts[0], DC_, FC_, NT_, TT_)
    P.fence()
    mla_latent_stage(cx, x1, gs[1], wd_d, wpe_d, gl_d,
                     cq_l.ap().rearrange("(p c) t -> p c t", c=4), ckv_l.ap().rearrange("(p c) t -> p c t", c=4),
                     kpe_l.ap().rearrange("(p a) t -> p a t", a=2), DC_, NT_)
    P.fence()
    _allgather(cx, cq_l, cq_ag); _allgather(cx, ckv_l, ckv_ag); _allgather(cx, kpe_l, kpe_ag)
    P.fence()
    cqv = cq_ag.ap().rearrange("(r p c) t -> r p c t", r=2, c=4)
    ckvv = ckv_ag.ap().rearrange("(r p c) t -> r p c t", r=2, c=4)
    kpev = kpe_ag.ap().rearrange("(r p a) t -> r p a t", r=2, a=2)
    mla_attn_stage(cx, [cqv[0], cqv[1]], [ckvv[0], ckvv[1]], [kpev[0], kpev[1]], rope_d, wuq_d, wuk_d, wuv_d,
                   gv_d, tri_d, lambda lh: o_l[lh // 2].ap().rearrange("(h p) t -> p h t", h=2)[:, lh % 2, :], 8, SEQ)
    P.fence()
    for k in range(4):
        _allgather(cx, o_l[k], o_ag[k])
    P.fence()
    ovs = [o_ag[k].ap().rearrange("(r h p) t -> r p h t", r=2, h=2) for k in range(4)]
    proj_res_stage(cx, x1, x2, [[(ovs[k][r], 2 * k) for k in range(4)] for r in range(2)], womla_d, DC_, 16, NT_,
                   sel_dram=sel_d)
    P.fence()
    ffn_stage(cx, x2, x3, gs[2], wins[1], wouts[1], DC_, FC_, NT_, TT_)
    P.fence()
    ffn_stage(cx, x3, x4, gs[3], wins[2], wouts[2], DC_, FC_, NT_, TT_)
    P.fence()
    norm_out_stage(cx, x4, gs[4], [xn_l[k].ap().rearrange("(c p) t -> p c t", c=4) for k in range(4)], DC_, NT_)
    P.fence()
    for k in range(4):
        _allgather(cx, xn_l[k], xn_ag[k])
    P.fence()
    xnv = [xn_ag[k].ap().rearrange("(r c p) t -> r p c t", r=2, c=4) for k in range(4)]
    dil_attn_stage(cx, lambda s: [xnv[k][s // 4][:, :, (s % 4) * 512:(s % 4 + 1) * 512] for k in range(4)],
                   wdil_d, gv2_d, bt_d, id_d,
                   lambda hs: o2_l[hs // 2].ap().rearrange("(h p) t -> p h t", h=2)[:, hs % 2, :], 4, SEQ, DC_)
    P.fence()
    for k in range(2):
        _allgather(cx, o2_l[k], o2_ag[k])
    P.fence()
    o2vs = [o2_ag[k].ap().rearrange("(r h p) t -> r p h t", r=2, h=2) for k in range(2)]
    proj_res_stage(cx, x4, x5, [[(o2vs[k][r], 2 * k) for k in range(2)] for r in range(2)], wodil_d, DC_, 8, NT_,
                   sel_dram=sel_d)
    P.fence()
    ffn_stage(cx, x5, x6, gs[5], wins[3], wouts[3], DC_, FC_, NT_, TT_)
    P.emit()
    return nc


def rope_tables_host():
    inv = (1.0 / (np.float32(10000.0) ** (np.arange(0, 64, 2, dtype=np.float32) / np.float32(64)))).astype(np.float32)
    ang = np.arange(SEQ, dtype=np.float32)[:, None] * inv[None, :]
    cos, sin = np.cos(ang).astype(np.float32), np.sin(ang).astype(np.float32)
    t = np.zeros((64, 2, SEQ), np.float32)
    t[0:32, 0, :] = cos.T
    t[32:64, 0, :] = cos.T
    t[0:32, 1, :] = -sin.T
    t[32:64, 1, :] = sin.T
    return t


def dil_bias_tables_host(h):
    slopes = (2.0 ** (-8.0 * np.arange(1, 25, dtype=np.float32) / 24)).astype(np.float32).reshape(3, 8)
    ik = np.arange(128)[:, None]
    c = np.arange(256)[None, :]
    dist = c - ik
    ok = (dist >= 0) & (dist <= 128)
    import ml_dtypes
    bt = np.zeros((12, 128, 2, 256), np.float32)
    inv_scale = np.float32(np.sqrt(128.0))
    for hs in range(4):
        for g in range(3):
            b = -slopes[g, 4 * h + hs] * np.float32(DIL_D[g]) * dist.astype(np.float32)
            b = np.where(ok, b, np.float32(-30000.0)) * inv_scale
            hi = b.astype(ml_dtypes.bfloat16).astype(np.float32)
            lo = (b - hi).astype(ml_dtypes.bfloat16).astype(np.float32)
            bt[hs * 3 + g, :, 0, :] = hi
            bt[hs * 3 + g, :, 1, :] = lo
    return bt


def kernel(x, ffn1_norm, ffn1_w_in, ffn1_w_out, mix_norm, ffn2_norm, ffn2_w_in, ffn2_w_out,
           mla_w_down, mla_g_cq, mla_g_ckv, mla_w_uq, mla_w_ukv, mla_g_qn, mla_g_kn, mla_w_o,
           dil_w_qkv, dil_g_qn, dil_g_kn, dil_w_o):
    f = lambda a: np.asarray(a, dtype=np.float32)
    x = f(x)
    swap = np.concatenate([np.arange(32, 64), np.arange(0, 32)])
    common = {}
    gl6 = [f(ffn1_norm)[0], f(mix_norm)[0], f(ffn2_norm)[0], f(ffn1_norm)[1], f(mix_norm)[1], f(ffn2_norm)[1]]
    for k in range(6):
        common["g%d" % k] = lay_vec(gl6[k])
    ffw = [(f(ffn1_w_in)[0], f(ffn1_w_out)[0]), (f(ffn2_w_in)[0], f(ffn2_w_out)[0]),
           (f(ffn1_w_in)[1], f(ffn1_w_out)[1]), (f(ffn2_w_in)[1], f(ffn2_w_out)[1])]
    for k in range(4):
        common["win%d" % k] = lay_win(ffw[k][0])
        common["wout%d" % k] = lay_wout(ffw[k][1])
    wdn = f(mla_w_down)[0]
    common["wd"] = np.ascontiguousarray(wdn[:, :1024].reshape(DC_, 128, 8, 128).transpose(2, 1, 0, 3))
    common["wpe"] = np.stack([lay_cols(wdn, 1024 + np.arange(64)), lay_cols(wdn, 1024 + swap)])
    common["gl"] = np.ascontiguousarray(np.concatenate([lay_vec(f(mla_g_cq)[0]), lay_vec(f(mla_g_ckv)[0])], axis=1))
    common["rope_t"] = rope_tables_host()
    wuq, wukv = f(mla_w_uq)[0], f(mla_w_ukv)[0]
    gq, gk = f(mla_g_qn)[0], f(mla_g_kn)[0]
    wuq_r, wuk_r, wuv_r = [], [], []
    for h in range(2):
        qs_, ks_ = [], []
        for lh in range(8):
            H = 8 * h + lh
            cols = np.concatenate([H * 192 + np.arange(128), H * 192 + 128 + np.arange(64), H * 192 + 128 + swap])
            qs_.append(lay_cols(wuq, cols))
            ks_.append(lay_cols(wukv, H * 256 + np.arange(128)))
        wuq_r.append(np.stack(qs_))
        wuk_r.append(np.stack(ks_))
        vs_ = []
        for hg in range(2):
            cols = np.concatenate([(8 * h + hg * 4 + i) * 256 + 128 + np.arange(128) for i in range(4)])
            vs_.append(lay_cols(wukv, cols))
        wuv_r.append(np.stack(vs_))
    gv = np.zeros((128, 6), np.float32)
    gv[:, 0] = gq[:128]; gv[:64, 1] = gq[128:]; gv[:64, 2] = gq[128 + swap]
    gv[:, 3] = gk[:128]; gv[:64, 4] = gk[128:]; gv[:64, 5] = gk[128 + swap]
    common["gv"] = gv
    common["tri"] = (np.arange(128)[None, :] >= np.arange(128)[:, None]).astype(np.float32)
    common["womla"] = lay_wout(f(mla_w_o)[0])
    wq = f(dil_w_qkv)[0].reshape(DC_, 128, 3, 3, 8, 128)
    w_r = []
    for h in range(2):
        w = wq[:, :, :, :, 4 * h:4 * h + 4, :]
        w_r.append(np.ascontiguousarray(w.transpose(4, 3, 2, 1, 0, 5).reshape(4, 9, 128, DC_, 128)))
    common["gv2"] = np.ascontiguousarray(np.stack([f(dil_g_qn)[0], f(dil_g_kn)[0]], axis=1))
    bt = [dil_bias_tables_host(h) for h in range(2)]
    common["ident"] = np.eye(128, dtype=np.float32)
    common["wodil"] = lay_wout(f(dil_w_o)[0])
    ims = []
    for c in range(NCORE):
        b, h = c // 2, c % 2
        m = dict(common)
        m["x_in"] = lay_x(x[b, h * NT_:(h + 1) * NT_, :])
        m["wuq"], m["wuk"], m["wuv"] = wuq_r[h], wuk_r[h], wuv_r[h]
        sel = np.zeros((128, 2), np.float32); sel[:, h] = 1.0
        m["sel"] = sel
        m["wdil"] = w_r[h]
        m["bt"] = bt[h]
        ims.append(m)
    nc = build_fused()
    res = _run(nc, ims)
    out = np.empty((BATCH, SEQ, D_MODEL), np.float32)
    for c in range(NCORE):
        out[c // 2, (c % 2) * NT_:(c % 2 + 1) * NT_, :] = unlay_x(res[c]["x6"])
    return out
```
